# Optimizing a Trainium2 kernel written in Bass

```python
import jax, jax.numpy as jnp
from jax import lax
import numpy as np

D_MODEL = 1024
BATCH = 1
SEQ = 16384
DEPTH = 2

N_Q_HEADS = 8
N_KV_HEADS = 2
HEAD_DIM = 64
Q_PER_KV = N_Q_HEADS // N_KV_HEADS
ATTN_WIDTH = N_Q_HEADS * HEAD_DIM
KV_WIDTH = N_KV_HEADS * HEAD_DIM
WINDOW = 128
ROT_DIM = HEAD_DIM // 4
ROPE_THETA = 500000.0
CONF_WIDTH = D_MODEL // 2
CONF_KERNEL = 31
E_IN_COLS = ATTN_WIDTH + 2 * KV_WIDTH + 2 * CONF_WIDTH
E_OUT_COLS = ATTN_WIDTH + CONF_WIDTH
SSM_EXPAND = 2
D_INNER = SSM_EXPAND * D_MODEL
SSM_HEADDIM = 64
SSM_HEADS = D_INNER // SSM_HEADDIM
SSM_GROUPS = 4
HEADS_PER_GROUP = SSM_HEADS // SSM_GROUPS
D_STATE = 128
SSM_CONV = 4
CHUNK = 128
XBC_WIDTH = D_INNER + 2 * SSM_GROUPS * D_STATE
O_IN_COLS = D_INNER + XBC_WIDTH + SSM_HEADS
D_FF = 2816
FFN_CONV = 3
N_EVEN = (DEPTH + 1) // 2
N_ODD = DEPTH // 2
NORM_EPS = 1e-6
LN_EPS = 1e-5

kernel_name = 'hybrid_swa_conformer_ssd_convffn_adaln'


def rmsnorm(x, g):
    xf = x.astype(jnp.float32)
    y = xf * lax.rsqrt(jnp.mean(xf * xf, axis=-1, keepdims=True) + NORM_EPS)
    return (y * g.astype(jnp.float32)).astype(x.dtype)


def causal_dwconv(u, w, b):
    k = w.shape[0]
    ch = u.shape[-1]
    out = lax.conv_general_dilated(u, w.astype(u.dtype)[:, None, :], (1,), ((k - 1, 0),),
                                   dimension_numbers=('NWC', 'WIO', 'NWC'),
                                   feature_group_count=ch)
    return out + b.astype(u.dtype)


def rope_partial(x, positions):
    half = ROT_DIM // 2
    inv_freq = 1.0 / (ROPE_THETA ** (jnp.arange(half, dtype=jnp.float32) * 2.0 / ROT_DIM))
    ang = positions.astype(jnp.float32)[..., None] * inv_freq
    cos = jnp.cos(ang)[:, :, None, :]
    sin = jnp.sin(ang)[:, :, None, :]
    xf = x.astype(jnp.float32)
    x1 = xf[..., :half]
    x2 = xf[..., half:ROT_DIM]
    return jnp.concatenate([x1 * cos - x2 * sin, x2 * cos + x1 * sin, xf[..., ROT_DIM:]], axis=-1)


def band_blocks(t):
    b, s = t.shape[:2]
    nb = s // WINDOW
    cur = t.reshape(b, nb, WINDOW, t.shape[2], t.shape[3])
    prev = jnp.pad(cur, ((0, 0), (1, 0), (0, 0), (0, 0), (0, 0)))[:, :-1]
    return jnp.concatenate([prev, cur], axis=2)


def swa_sink_attention(q, k, v, sinks, positions):
    b, s = q.shape[:2]
    nb = s // WINDOW
    qb = rope_partial(q, positions).reshape(b, nb, WINDOW, N_KV_HEADS, Q_PER_KV, HEAD_DIM)
    kb = band_blocks(rope_partial(k, positions))
    vb = band_blocks(v.astype(jnp.float32))
    scores = jnp.einsum('bnqhgd,bnkhd->bnhgqk', qb, kb) * (HEAD_DIM ** -0.5)
    qi = jnp.arange(WINDOW)[:, None]
    kj = jnp.arange(2 * WINDOW)[None, :]
    band = (kj > qi) & (kj <= qi + WINDOW)
    key_pos = jnp.arange(nb)[:, None, None] * WINDOW + kj[None] - WINDOW
    mask = band[None] & (key_pos >= 0)
    scores = jnp.where(mask[None, :, None, None], scores, -jnp.inf)
    sink = sinks.astype(jnp.float32).reshape(1, 1, N_KV_HEADS, Q_PER_KV, 1, 1)
    m = jnp.maximum(scores.max(axis=-1, keepdims=True), sink)
    p = jnp.exp(scores - m)
    denom = p.sum(axis=-1, keepdims=True) + jnp.exp(sink - m)
    out = jnp.einsum('bnhgqk,bnkhd->bnqhgd', p / denom, vb)
    return out.reshape(b, s, ATTN_WIDTH)


def conformer_conv(u, conv_w, conv_b, ln_g, ln_b):
    a, g = jnp.split(u, 2, axis=-1)
    h = causal_dwconv(a * jax.nn.sigmoid(g), conv_w, conv_b).astype(jnp.float32)
    mu = jnp.mean(h, axis=-1, keepdims=True)
    var = jnp.mean(jnp.square(h - mu), axis=-1, keepdims=True)
    h = (h - mu) * lax.rsqrt(var + LN_EPS) * ln_g.astype(jnp.float32) + ln_b.astype(jnp.float32)
    return jax.nn.silu(h)


def even_mixer(h, positions, w_in, b_in, sinks, conv_w, conv_b, ln_g, ln_b, w_out):
    b, s = h.shape[:2]
    proj = h @ w_in + b_in
    q, k, v, conf_in = jnp.split(proj, [ATTN_WIDTH, ATTN_WIDTH + KV_WIDTH,
                                        ATTN_WIDTH + 2 * KV_WIDTH], axis=-1)
    q = q.reshape(b, s, N_Q_HEADS, HEAD_DIM)
    k = k.reshape(b, s, N_KV_HEADS, HEAD_DIM)
    v = v.reshape(b, s, N_KV_HEADS, HEAD_DIM)
    attn = swa_sink_attention(q, k, v, sinks, positions)
    conf = conformer_conv(conf_in, conv_w, conv_b, ln_g, ln_b)
    merged = jnp.concatenate([attn, conf], axis=-1).astype(h.dtype)
    return merged @ w_out


def ssd_chunked(xdt, a, bm, cm):
    b, s = xdt.shape[:2]
    nc = s // CHUNK
    xc = xdt.reshape(b, nc, CHUNK, SSM_GROUPS, HEADS_PER_GROUP, SSM_HEADDIM)
    ac = a.reshape(b, nc, CHUNK, SSM_GROUPS, HEADS_PER_GROUP)
    bc = bm.reshape(b, nc, CHUNK, SSM_GROUPS, D_STATE)
    cc = cm.reshape(b, nc, CHUNK, SSM_GROUPS, D_STATE)
    a_cs = jnp.cumsum(ac, axis=2)
    causal = jnp.tril(jnp.ones((CHUNK, CHUNK), dtype=bool))
    seg = a_cs[:, :, :, None] - a_cs[:, :, None, :]
    decay_in = jnp.exp(jnp.where(causal[:, :, None, None], seg, -jnp.inf))
    cb = jnp.einsum('bclgn,bcsgn->bclsg', cc, bc)
    y_diag = jnp.einsum('bclsgj,bcsgjp->bclgjp', cb[..., None] * decay_in, xc)
    decay_to_end = jnp.exp(a_cs[:, :, -1:] - a_cs)
    states = jnp.einsum('bcsgn,bcsgjp->bcgjpn', bc, xc * decay_to_end[..., None])
    chunk_decay = jnp.exp(a_cs[:, :, -1])

    def step(state, inp):
        dec, st = inp
        return dec[..., None, None] * state + st, state

    h0 = jnp.zeros((b, SSM_GROUPS, HEADS_PER_GROUP, SSM_HEADDIM, D_STATE), jnp.float32)
    _, prev = lax.scan(step, h0, (jnp.moveaxis(chunk_decay, 1, 0), jnp.moveaxis(states, 1, 0)))
    prev = jnp.moveaxis(prev, 0, 1)
    y_off = jnp.einsum('bclgn,bcgjpn->bclgjp', cc, prev) * jnp.exp(a_cs)[..., None]
    return (y_diag + y_off).reshape(b, s, SSM_GROUPS, HEADS_PER_GROUP, SSM_HEADDIM)


def ssd_mixer(h, w_in, conv_w, conv_b, dt_bias, a_log, d_skip, norm_g, w_out):
    b, s = h.shape[:2]
    proj = h @ w_in
    z, xbc, dt = jnp.split(proj, [D_INNER, D_INNER + XBC_WIDTH], axis=-1)
    xbc = jax.nn.silu(causal_dwconv(xbc, conv_w, conv_b))
    xs, bm, cm = jnp.split(xbc, [D_INNER, D_INNER + SSM_GROUPS * D_STATE], axis=-1)
    xs = xs.astype(jnp.float32).reshape(b, s, SSM_GROUPS, HEADS_PER_GROUP, SSM_HEADDIM)
    bm = bm.astype(jnp.float32).reshape(b, s, SSM_GROUPS, D_STATE)
    cm = cm.astype(jnp.float32).reshape(b, s, SSM_GROUPS, D_STATE)
    dt = jax.nn.softplus(dt.astype(jnp.float32) + dt_bias.astype(jnp.float32))
    dt = dt.reshape(b, s, SSM_GROUPS, HEADS_PER_GROUP)
    a = -jnp.exp(a_log.astype(jnp.float32)).reshape(SSM_GROUPS, HEADS_PER_GROUP)
    y = ssd_chunked(xs * dt[..., None], dt * a, bm, cm)
    y = y + xs * d_skip.astype(jnp.float32).reshape(SSM_GROUPS, HEADS_PER_GROUP, 1)
    y = y.reshape(b, s, D_INNER) * jax.nn.silu(z.astype(jnp.float32))
    yg = y.reshape(b, s, SSM_GROUPS, D_INNER // SSM_GROUPS)
    yg = yg * lax.rsqrt(jnp.mean(yg * yg, axis=-1, keepdims=True) + NORM_EPS)
    y = yg.reshape(b, s, D_INNER) * norm_g.astype(jnp.float32)
    return y.astype(h.dtype) @ w_out


def conv_ffn(h, w_gate, conv_w, conv_b, w_val, w_down):
    g = causal_dwconv(h @ w_gate, conv_w, conv_b)
    return (jax.nn.silu(g) * (h @ w_val)) @ w_down


def setup_inputs(seed: int = 0) -> dict:
    key = jax.random.key(seed)
    ks = iter(jax.random.split(key, 40))

    def nrm(shape, scale):
        return jax.random.normal(next(ks), shape, jnp.float32) * scale

    def gain(shape):
        return 1.0 + nrm(shape, 0.02)

    x = nrm((BATCH, SEQ, D_MODEL), 1.0)
    c = nrm((BATCH, D_MODEL), 1.0)
    positions = jnp.broadcast_to(jnp.arange(SEQ, dtype=jnp.int32)[None, :], (BATCH, SEQ))
    w_mod = nrm((DEPTH, D_MODEL, 6 * D_MODEL), 0.5 * D_MODEL ** -0.5)
    b_mod = nrm((DEPTH, 6 * D_MODEL), 0.01)
    norm_mix = gain((DEPTH, D_MODEL))
    norm_ffn = gain((DEPTH, D_MODEL))
    w_in_e = nrm((N_EVEN, D_MODEL, E_IN_COLS), D_MODEL ** -0.5)
    b_in_e = nrm((N_EVEN, E_IN_COLS), 0.01)
    attn_sinks = nrm((N_EVEN, N_Q_HEADS), 1.0)
    conf_conv_w = nrm((N_EVEN, CONF_KERNEL, CONF_WIDTH), CONF_KERNEL ** -0.5)
    conf_conv_b = nrm((N_EVEN, CONF_WIDTH), 0.01)
    conf_ln_g = gain((N_EVEN, CONF_WIDTH))
    conf_ln_b = nrm((N_EVEN, CONF_WIDTH), 0.01)
    w_out_e = nrm((N_EVEN, E_OUT_COLS, D_MODEL), E_OUT_COLS ** -0.5)
    w_in_o = nrm((N_ODD, D_MODEL, O_IN_COLS), D_MODEL ** -0.5)
    ssm_conv_w = nrm((N_ODD, SSM_CONV, XBC_WIDTH), SSM_CONV ** -0.5)
    ssm_conv_b = nrm((N_ODD, XBC_WIDTH), 0.01)
    dt0 = jnp.exp(jax.random.uniform(next(ks), (N_ODD, SSM_HEADS), jnp.float32,
                                     np.log(1e-3).astype(np.float32), np.log(1e-1).astype(np.float32)))
    ssm_dt_bias = dt0 + jnp.log(-jnp.expm1(-dt0))
    ssm_a_log = jnp.log(jax.random.uniform(next(ks), (N_ODD, SSM_HEADS), jnp.float32, 1.0, 16.0))
    ssm_d = 1.0 + nrm((N_ODD, SSM_HEADS), 0.1)
    ssm_norm_g = gain((N_ODD, D_INNER))
    w_out_o = nrm((N_ODD, D_INNER, D_MODEL), D_INNER ** -0.5)
    ffn_w_gate = nrm((DEPTH, D_MODEL, D_FF), D_MODEL ** -0.5)
    ffn_conv_w = nrm((DEPTH, FFN_CONV, D_FF), FFN_CONV ** -0.5)
    ffn_conv_b = nrm((DEPTH, D_FF), 0.01)
    ffn_w_val = nrm((DEPTH, D_MODEL, D_FF), D_MODEL ** -0.5)
    ffn_w_down = nrm((DEPTH, D_FF, D_MODEL), D_FF ** -0.5)
    final_norm = gain((D_MODEL,))
    return {'x': x, 'c': c, 'positions': positions, 'w_mod': w_mod, 'b_mod': b_mod,
            'norm_mix': norm_mix, 'norm_ffn': norm_ffn, 'w_in_e': w_in_e, 'b_in_e': b_in_e,
            'attn_sinks': attn_sinks, 'conf_conv_w': conf_conv_w, 'conf_conv_b': conf_conv_b,
            'conf_ln_g': conf_ln_g, 'conf_ln_b': conf_ln_b, 'w_out_e': w_out_e,
            'w_in_o': w_in_o, 'ssm_conv_w': ssm_conv_w, 'ssm_conv_b': ssm_conv_b,
            'ssm_dt_bias': ssm_dt_bias, 'ssm_a_log': ssm_a_log, 'ssm_d': ssm_d,
            'ssm_norm_g': ssm_norm_g, 'w_out_o': w_out_o, 'ffn_w_gate': ffn_w_gate,
            'ffn_conv_w': ffn_conv_w, 'ffn_conv_b': ffn_conv_b, 'ffn_w_val': ffn_w_val,
            'ffn_w_down': ffn_w_down, 'final_norm': final_norm}


def reference(x, c, positions, w_mod, b_mod, norm_mix, norm_ffn, w_in_e, b_in_e, attn_sinks,
              conf_conv_w, conf_conv_b, conf_ln_g, conf_ln_b, w_out_e, w_in_o, ssm_conv_w,
              ssm_conv_b, ssm_dt_bias, ssm_a_log, ssm_d, ssm_norm_g, w_out_o, ffn_w_gate,
              ffn_conv_w, ffn_conv_b, ffn_w_val, ffn_w_down, final_norm):
    silu_c = jax.nn.silu(c)
    for i in range(DEPTH):
        mod = (silu_c @ w_mod[i] + b_mod[i])[:, None, :]
        sh_m, sc_m, g_m, sh_f, sc_f, g_f = jnp.split(mod, 6, axis=-1)
        h = rmsnorm(x, norm_mix[i]) * (1.0 + sc_m) + sh_m
        j = i // 2
        if i % 2 == 0:
            y = even_mixer(h, positions, w_in_e[j], b_in_e[j], attn_sinks[j], conf_conv_w[j],
                           conf_conv_b[j], conf_ln_g[j], conf_ln_b[j], w_out_e[j])
        else:
            y = ssd_mixer(h, w_in_o[j], ssm_conv_w[j], ssm_conv_b[j], ssm_dt_bias[j],
                          ssm_a_log[j], ssm_d[j], ssm_norm_g[j], w_out_o[j])
        x = x + g_m * y
        h = rmsnorm(x, norm_ffn[i]) * (1.0 + sc_f) + sh_f
        x = x + g_f * conv_ffn(h, ffn_w_gate[i], ffn_conv_w[i], ffn_conv_b[i],
                               ffn_w_val[i], ffn_w_down[i])
    return rmsnorm(x, final_norm)
```

```python
import numpy as np
import concourse.bass as bass
import concourse.mybir as mybir

F32 = mybir.dt.float32
BF16 = mybir.dt.bfloat16
I32 = mybir.dt.int32
AF = mybir.ActivationFunctionType
ALU = mybir.AluOpType
AX = mybir.AxisListType

_DSZ = {F32: 4, BF16: 2, I32: 4}


def _dsz(dt):
    if dt in _DSZ:
        return _DSZ[dt]
    return mybir.dt.size(dt) if hasattr(mybir.dt, "size") else 4


class Prog:
    NDMA_SEMS = 6

    def __init__(self, nc):
        self.nc = nc
        self.engs = ["pe", "act", "dve", "pool", "sp"]
        self.ops = {e: [] for e in self.engs}
        self.seq = {e: 0 for e in self.engs}
        self.waited = {e: {} for e in self.engs}
        self.recs = {}
        self.dma_i = {"sp": 0, "pool": 0, "act": 0}
        self.sems = {}
        self.final_events = []

    @staticmethod
    def region(ap):
        t = ap.tensor
        name = t.name
        es = _dsz(ap.dtype)
        dims = [(int(s) * es, int(c)) for s, c in ap.ap]
        off = int(ap.offset) * es
        tn = type(t).__name__
        if tn.startswith("DRam"):
            p0, p1 = 0, 1
            fd = dims
            foff = off
        else:
            fsz = _dsz(t.dtype)
            for d in list(t.shape)[1:]:
                fsz *= int(d)
            pstep, pcnt = dims[0]
            p0 = off // fsz
            p1 = p0 + pcnt
            foff = off % fsz
            fd = dims[1:]
        fd = [(s, c) for s, c in fd if c > 1 and s != 0]
        fd.sort(key=lambda sc: -abs(sc[0]))
        if not fd:
            lo, hi, S, w, n = foff, foff + es, 0, es, 1
        elif len(fd) == 1:
            s, c = fd[0]
            if s == es:
                lo, hi, S, w, n = foff, foff + c * es, 0, c * es, 1
            else:
                lo, hi, S, w, n = foff, foff + (c - 1) * s + es, s, es, c
        else:
            inner = fd[1:]
            w = sum((c - 1) * s for s, c in inner) + es
            S, n = fd[0]
            lo = foff
            hi = foff + (n - 1) * S + w
            if w > S:
                S, w, n = 0, hi - lo, 1
        return (name, p0, p1, lo, hi, S, w, n)

    @staticmethod
    def overlap(a, b):
        if a[2] <= b[1] or b[2] <= a[1]:
            return False
        if a[4] <= b[3] or b[4] <= a[3]:
            return False
        Sa, Sb = a[5], b[5]
        if Sa and Sa == Sb:
            S = Sa
            x = a[3] % S
            y = b[3] % S
            wa, wb = a[6], b[6]
            d = (y - x) % S
            if d < wa:
                return True
            d2 = (x - y) % S
            if d2 < wb:
                return True
            return False
        return True

    def _event_for(self, eng, is_dma):
        if is_dma:
            i = self.dma_i[eng]
            self.dma_i[eng] += 1
            k = i % self.NDMA_SEMS
            return ("dma_%s_%d" % (eng, k), 16 * (i // self.NDMA_SEMS + 1), 16)
        self.seq[eng] += 1
        return ("prog_" + eng, self.seq[eng], 1)

    def op(self, eng, fn, reads, writes, is_dma=False):
        deps = {}
        rr = [self.region(a) for a in reads]
        wr = [self.region(a) for a in writes]
        for r in rr:
            for rec in self.recs.get(r[0], ()):
                if rec[1] == "w" and self.overlap(r, rec[0]):
                    s, v = rec[2]
                    if deps.get(s, 0) < v:
                        deps[s] = v
        for r in wr:
            for rec in self.recs.get(r[0], ()):
                if self.overlap(r, rec[0]):
                    s, v = rec[2]
                    if deps.get(s, 0) < v:
                        deps[s] = v
        sem, val, inc = self._event_for(eng, is_dma)
        if eng == "pe":
            deps.pop("prog_pe", None)
        waits = []
        wd = self.waited[eng]
        for s, v in deps.items():
            if wd.get(s, 0) >= v:
                continue
            wd[s] = v
            waits.append((s, v))
        self.ops[eng].append((waits, fn, sem, inc))
        ev = (sem, val)
        for r in wr:
            lst = self.recs.setdefault(r[0], [])
            keep = []
            for rec in lst:
                q = rec[0]
                cov = (r[1] <= q[1] and q[2] <= r[2] and
                       ((r[5] == 0 and r[3] <= q[3] and q[4] <= r[4]) or q[1:] == r[1:] or
                        (r[5] == q[5] and r[6] == q[6] and r[3] == q[3] and q[7] <= r[7])))
                if not cov:
                    keep.append(rec)
            keep.append((r, "w", ev))
            self.recs[r[0]] = keep
        for r in rr:
            lst = self.recs.setdefault(r[0], [])
            keep = [rec for rec in lst if not (rec[1] == "r" and rec[2][0] == sem and rec[0] == r)]
            keep.append((r, "r", ev))
            self.recs[r[0]] = keep
        return ev

    def dma(self, out, in_, q="sp", **kw):
        def fn(e):
            return e.dma_start(out=out, in_=in_, **kw)
        return self.op(q, fn, [in_], [out], is_dma=True)

    def matmul(self, out, lhsT, rhs, start=True, stop=True, **kw):
        def fn(e):
            return e.matmul(out, lhsT, rhs, start=start, stop=stop, **kw)
        return self.op("pe", fn, [lhsT, rhs], [out])

    def transpose(self, out, in_, ident):
        def fn(e):
            return e.transpose(out, in_, ident)
        return self.op("pe", fn, [in_, ident], [out])

    def act(self, out, in_, func, bias=None, scale=None, accum_out=None, eng="act"):
        reads = [in_]
        kw = {}
        if bias is not None:
            kw["bias"] = bias
            if not isinstance(bias, (int, float)):
                reads.append(bias)
        if scale is not None:
            kw["scale"] = scale
            if not isinstance(scale, (int, float)):
                reads.append(scale)
        writes = [out]
        if accum_out is not None:
            kw["accum_out"] = accum_out
            writes.append(accum_out)

        def fn(e):
            return e.activation(out, in_, func, **kw)
        return self.op(eng, fn, reads, writes)

    def tt(self, out, in0, in1, op, eng="dve"):
        def fn(e):
            return e.tensor_tensor(out, in0, in1, op)
        return self.op(eng, fn, [in0, in1], [out])

    def ts(self, out, in0, s1, s2, op0, op1=None, accum_out=None, eng="dve"):
        reads = [in0]
        for s in (s1, s2):
            if s is not None and not isinstance(s, (int, float)):
                reads.append(s)
        writes = [out] + ([accum_out] if accum_out is not None else [])

        def fn(e):
            kw = {}
            if accum_out is not None:
                kw["accum_out"] = accum_out
            if op1 is None:
                return e.tensor_scalar(out, in0, s1, None, op0, **kw)
            return e.tensor_scalar(out, in0, s1, s2, op0, op1, **kw)
        return self.op(eng, fn, reads, writes)

    def stt(self, out, in0, scalar, in1, op0, op1, accum_out=None, eng="dve"):
        reads = [in0, in1]
        if not isinstance(scalar, (int, float)):
            reads.append(scalar)
        writes = [out] + ([accum_out] if accum_out is not None else [])

        def fn(e):
            kw = {}
            if accum_out is not None:
                kw["accum_out"] = accum_out
            return e.scalar_tensor_tensor(out, in0, scalar, in1, op0, op1, **kw)
        return self.op(eng, fn, reads, writes)

    def copy(self, out, in_, eng="dve"):
        if eng == "act":
            def fn(e):
                return e.copy(out, in_)
        else:
            def fn(e):
                return e.tensor_copy(out, in_)
        return self.op(eng, fn, [in_], [out])

    def memset(self, out, val, eng="dve"):
        def fn(e):
            return e.memset(out, val)
        return self.op(eng, fn, [], [out])

    def reduce(self, out, in_, op, axis=None, eng="dve"):
        axis = axis or AX.X

        def fn(e):
            return e.tensor_reduce(out, in_, axis, op)
        return self.op(eng, fn, [in_], [out])

    def recip(self, out, in_, eng="dve"):
        def fn(e):
            return e.reciprocal(out, in_)
        return self.op(eng, fn, [in_], [out])

    def generic(self, eng, fn, reads, writes):
        return self.op(eng, fn, reads, writes)

    def must_finish(self, ev):
        self.final_events.append(ev)

    def emit(self):
        nc = self.nc
        names = set()
        for e in self.engs:
            for waits, fn, sem, inc in self.ops[e]:
                names.add(sem)
                for s, v in waits:
                    names.add(s)
        for s, v in self.final_events:
            names.add(s)
        names = sorted(names)
        import contextlib
        with contextlib.ExitStack() as st:
            sems = {n: st.enter_context(nc.semaphore(n)) for n in names}
            block = st.enter_context(nc.Block())
            engmap = {"pe": block.tensor, "act": block.scalar, "dve": block.vector,
                      "pool": block.gpsimd, "sp": block.sync}
            finals = list(self.final_events)

            def make(ename):
                lst = self.ops[ename]

                def body(e):
                    for waits, fn, sem, inc in lst:
                        for s, v in waits:
                            e.wait_ge(sems[s], v)
                        fn(e).then_inc(sems[sem], inc)
                    if ename == "sp":
                        for s, v in finals:
                            e.wait_ge(sems[s], v)
                return body
            for ename in self.engs:
                if self.ops[ename] or ename == "sp":
                    engmap[ename](make(ename))

import contextlib
from concourse.bass_utils import run_bass_kernel_spmd

NCORES = 8
SEQ = 16384
D = 1024
TC = SEQ // NCORES
NT = 1024
NV = TC // NT
HK = 256
HA = 8
WK = HK + NT
WA = HA + NT
QW = 128 + NT
GW = 1064
G0 = WK - GW
DFF = 2816
NJ = DFF // 128
NEG = -30000.0
WRING = 4
WELEMS = 4096

_SV = {}
_SVN = 0


def _sv(name, n):
    global _SVN
    _SV[name] = (_SVN, n)
    _SVN += n


for _l in range(2):
    _sv("bmod%d" % _l, 48)
    _sv("nmix%d" % _l, 8)
    _sv("nffn%d" % _l, 8)
    _sv("fcw%d" % _l, NJ * 3)
    _sv("fcb%d" % _l, NJ)
_sv("bqk", 10)
_sv("bca", 4)
_sv("bcg", 4)
_sv("ccw", 4 * 31)
_sv("ccb", 4)
_sv("lng", 4)
_sv("lnb", 4)
_sv("scw", 24 * 4)
_sv("scb", 24)
_sv("fnorm", 8)
_sv("invf", 1)
_sv("sgn", 1)
_sv("flag", NV)
_sv("ngc", 16)
_sv("one", 1)
_sv("msk", 72)
NSV = _SVN

_RV = {}
_RVN = 0


def _rv(name, n):
    global _RVN
    _RV[name] = (_RVN, n)
    _RVN += n


_rv("bv", 128)
_rv("sinks", 8)
_rv("dtb", 32)
_rv("alog", 32)
_rv("dsk", 32)
NRV = _RVN


def _colmajor(v, p=128):
    v = np.asarray(v, np.float32)
    return np.ascontiguousarray(v.reshape(-1, p).T)


def _blk(w, cols, kc=8):
    sub = w[:, cols]
    return np.ascontiguousarray(sub.reshape(kc, 128, len(cols)).transpose(1, 0, 2).reshape(128, -1))


def pack_weights(inp):
    W = {}
    wie = np.asarray(inp["w_in_e"][0], np.float32)
    heads = []
    for h in range(10):
        c0 = h * 64 if h < 8 else 512 + (h - 8) * 64
        heads.append(_blk(wie, list(range(c0, c0 + 64))))
    W["wqk"] = np.stack([np.concatenate(heads[0:5], 1), np.concatenate(heads[5:10], 1)])
    W["wv"] = _blk(wie, list(range(640, 768)))[None]
    ca = [_blk(wie, list(range(768 + c * 128, 768 + (c + 1) * 128))) for c in range(4)]
    cg = [_blk(wie, list(range(1280 + c * 128, 1280 + (c + 1) * 128))) for c in range(4)]
    W["wcf"] = np.stack([np.concatenate([ca[0], cg[0], ca[1], cg[1]], 1),
                         np.concatenate([ca[2], cg[2], ca[3], cg[3]], 1)])
    woe = np.asarray(inp["w_out_e"][0], np.float32)
    wo = np.zeros((8, 128, 1536), np.float32)
    for o in range(8):
        for h in range(8):
            wo[o, 0:64, h * 128:(h + 1) * 128] = woe[h * 64:(h + 1) * 64, o * 128:(o + 1) * 128]
        for c in range(4):
            wo[o, :, 1024 + c * 128:1024 + (c + 1) * 128] = woe[512 + c * 128:512 + (c + 1) * 128, o * 128:(o + 1) * 128]
    W["wo"] = np.ascontiguousarray(wo.reshape(4, 2, 128, 1536).transpose(0, 2, 1, 3).reshape(4, 128, 3072))
    for l in range(2):
        wg = np.asarray(inp["ffn_w_gate"][l], np.float32)
        wvv = np.asarray(inp["ffn_w_val"][l], np.float32)
        wd = np.asarray(inp["ffn_w_down"][l], np.float32)
        gv = []
        for j in range(0, NJ, 2):
            parts = []
            for jj in (j, j + 1):
                cols = list(range(jj * 128, (jj + 1) * 128))
                parts += [_blk(wg, cols), _blk(wvv, cols)]
            gv.append(np.concatenate(parts, 1))
        W["wgv%d" % l] = np.stack(gv)
        dn = []
        for hf in range(2):
            wdh = wd[hf * 1408:(hf + 1) * 1408]
            for o in range(0, 8, 2):
                parts = [_blk(wdh, list(range(oo * 128, (oo + 1) * 128)), kc=11) for oo in (o, o + 1)]
                dn.append(np.concatenate(parts, 1))
        W["wdn%d" % l] = np.stack(dn)
        wm = np.asarray(inp["w_mod"][l], np.float32)
        wmt = wm.T.reshape(12, 4, 128, 1024).transpose(0, 2, 1, 3).reshape(12, 128, 4096)
        W["wmod%d" % l] = np.ascontiguousarray(wmt)
    wio = np.asarray(inp["w_in_o"][0], np.float32)
    W["wz"] = np.stack([_blk(wio, list(range(t * 512, (t + 1) * 512))) for t in range(4)])
    xb = [_blk(wio, list(range(2048 + c * 128, 2048 + (c + 1) * 128))) for c in range(24)]
    W["wxbc"] = np.stack([np.concatenate(xb[i * 4:(i + 1) * 4], 1) for i in range(6)])
    W["wdt"] = _blk(wio, list(range(5120, 5152)))[None]
    woo = np.asarray(inp["w_out_o"][0], np.float32)
    oo_ = [_blk(woo, list(range(o * 128, (o + 1) * 128)), kc=16) for o in range(8)]
    W["woo"] = np.stack([np.concatenate(oo_[i * 2:(i + 1) * 2], 1) for i in range(4)])

    sv = np.zeros((128, NSV), np.float32)

    def put(name, arr):
        a, n = _SV[name]
        arr = np.asarray(arr, np.float32)
        assert arr.shape[1] == n, (name, arr.shape, n)
        sv[:arr.shape[0], a:a + n] = arr

    for l in range(2):
        put("bmod%d" % l, _colmajor(inp["b_mod"][l]))
        put("nmix%d" % l, _colmajor(inp["norm_mix"][l]))
        put("nffn%d" % l, _colmajor(inp["norm_ffn"][l]))
        fw_ = np.asarray(inp["ffn_conv_w"][l], np.float32)
        put("fcw%d" % l, fw_.T.reshape(NJ, 128, 3).transpose(1, 0, 2).reshape(128, NJ * 3))
        put("fcb%d" % l, _colmajor(inp["ffn_conv_b"][l]))
    bie = np.asarray(inp["b_in_e"][0], np.float32)
    put("bqk", np.concatenate([_colmajor(bie[0:512], 64), _colmajor(bie[512:640], 64)], 1))
    put("bca", _colmajor(bie[768:1280]))
    put("bcg", _colmajor(bie[1280:1792]))
    cw = np.asarray(inp["conf_conv_w"][0], np.float32)
    put("ccw", cw.T.reshape(4, 128, 31).transpose(1, 0, 2).reshape(128, 124))
    put("ccb", _colmajor(inp["conf_conv_b"][0]))
    put("lng", _colmajor(inp["conf_ln_g"][0]))
    put("lnb", _colmajor(inp["conf_ln_b"][0]))
    sw = np.asarray(inp["ssm_conv_w"][0], np.float32)
    put("scw", sw.T.reshape(24, 128, 4).transpose(1, 0, 2).reshape(128, 96))
    put("scb", _colmajor(inp["ssm_conv_b"][0]))
    put("fnorm", _colmajor(inp["final_norm"]))
    half = 8
    inv_freq = (1.0 / (500000.0 ** (np.arange(half, dtype=np.float32) * 2.0 / 16.0))).astype(np.float32)
    invf = np.zeros((128, 1), np.float32)
    invf[0:16, 0] = np.concatenate([inv_freq, inv_freq])
    put("invf", invf)
    sg = np.zeros((128, 1), np.float32)
    sg[0:8] = -1.0
    sg[8:16] = 1.0
    put("sgn", sg)
    put("ngc", _colmajor(inp["ssm_norm_g"][0]))
    put("one", np.ones((128, 1), np.float32))
    W["sv"] = sv
    rv = np.zeros((1, NRV), np.float32)

    def putr(name, arr):
        a, n = _RV[name]
        rv[0, a:a + n] = np.asarray(arr, np.float32).reshape(-1)

    putr("bv", bie[640:768])
    putr("sinks", inp["attn_sinks"][0])
    putr("dtb", inp["ssm_dt_bias"][0])
    putr("alog", inp["ssm_a_log"][0])
    putr("dsk", inp["ssm_d"][0])
    W["rv"] = rv
    W["crow"] = np.asarray(inp["c"], np.float32).reshape(1, 1024)
    W["ngrow"] = np.asarray(inp["ssm_norm_g"], np.float32).reshape(1, 2048)
    qi = np.arange(128)[:, None]
    kj = np.arange(256)[None, :]
    band = (kj > qi) & (kj <= qi + 128)
    maskA = np.where(band, 0.0, NEG).astype(np.float32)
    cm = np.zeros((128, 1024), np.float32)
    cm[:, 0:256] = maskA
    cm[0:8, 256:512] = maskA[120:128]
    rp = np.zeros((64, 16), np.float32)
    for i in range(16):
        rp[(i + 8) % 16, i] = 1.0
    cm[0:64, 512:528] = rp
    s_ = np.arange(128)[:, None]
    l_ = np.arange(128)[None, :]
    cm[:, 640:768] = (s_ <= l_).astype(np.float32)
    cm[:, 768:896] = np.where(s_ > l_, NEG, 0.0)
    cm[:, 896:1024] = np.eye(128, dtype=np.float32)
    W["cm"] = cm
    return W


class Ctx:
    pass


def _tiles(n, step=512):
    out = []
    s = 0
    while s < n:
        out.append((s, min(step, n - s)))
        s += step
    return out


class LazyDram(dict):
    def __init__(self, nc, shapes):
        super().__init__()
        self.nc = nc
        self.shapes = shapes

    def __missing__(self, k):
        shp, dt = self.shapes[k]
        ap = self.nc.dram_tensor(k, list(shp), dt, kind="ExternalInput").ap()
        self[k] = ap
        return ap


def build_program(phases, debug=False):
    nc = bass.Bass("TRN2", target_bir_lowering=False)
    P = Prog(nc)
    C = Ctx()
    C.nc, C.P = nc, P
    C.debug = debug

    def dout(name, shape, dt=F32):
        return nc.dram_tensor(name, list(shape), dt, kind="ExternalOutput").ap()

    ishapes = dict(xT=([D, HK + TC], F32), pos=([1, HK + TC], I32), sv=([128, NSV], F32), rv=([1, NRV], F32),
                   cm=([128, 1024], F32), crow=([1, 1024], F32),
                   xw_in=([128, 8 * NV * WA], F32), Sall=([NCORES, 128, 2048], F32), Lall=([NCORES, 128, 32], F32),
                   tail_in=([128, 64], F32))
    wshapes = dict(wqk=[2, 128, 2560], wv=[1, 128, 1024], wcf=[2, 128, 4096], wo=[4, 128, 3072],
                   wgv0=[11, 128, 4096], wgv1=[11, 128, 4096], wdn0=[8, 128, 2816], wdn1=[8, 128, 2816],
                   wmod0=[12, 128, 4096], wmod1=[12, 128, 4096], wz=[4, 128, 4096], wxbc=[6, 128, 4096],
                   wdt=[1, 128, 256], woo=[4, 128, 4096])
    for k, sshape in wshapes.items():
        ishapes[k] = (sshape, F32)
    dr = LazyDram(nc, ishapes)
    C.dr = dr

    with contextlib.ExitStack() as st:
        def sb(name, shape, dt=F32):
            return st.enter_context(nc.sbuf_tensor(name, list(shape), dt))

        def psum(name, shape, dt=F32):
            return st.enter_context(nc.psum_tensor(name, list(shape), dt))

        C.xres = sb("xres", [128, 8, NV, WA])
        C.arena = sb("arena", [128, 16896])
        C.wring = [sb("wr%d" % i, [128, WELEMS], BF16) for i in range(WRING)]
        C.wri = 0
        C.svt = sb("svt", [128, NSV])
        C.rvt = sb("rvt", [128, NRV])
        C.cmf = sb("cmf", [128, 512])
        C.cmb = sb("cmb", [128, 1024], BF16)
        C.onesb = sb("onesb", [128, 128], BF16)
        C.modv = sb("modv", [128, 2, 48])
        C.modA = sb("modA", [128, 2, 32])
        C.t512 = [sb("t512_%d" % i, [128, 512]) for i in range(4)]
        C.b512 = [sb("b512_%d" % i, [128, 4, 512], BF16) for i in range(2)]
        C.aux = sb("aux", [128, 2 * WK])
        C.rope = C.aux[0:16, :].rearrange("p (a b) -> p a b", b=WK)
        C.small = sb("small", [128, 64])
        C.statA = sb("statA", [128, 512])
        C.statB = sb("statB", [128, 512])
        C.negc = sb("negc", [128, NV])
        C.epsc = sb("epsc", [128, 2])
        C.smi = 0
        C.psf = [psum("psf%d" % i, [128, 512]) for i in range(6)]
        C.psb = [psum("psb%d" % i, [128, 1024], BF16) for i in range(2)]
        C.psfi = 0
        C.psbi = 0
        C.ti = 0
        xflat = C.xres[:].rearrange("p k v i -> p (k v i)")

        emit_setup(C)
        if "L0" in phases:
            emit_mod(C, 0)
            C.stop_after_mixer = ("MIXONLY" in phases)
            for v in range(NV):
                emit_l0_tile(C, v)
        else:
            P.dma(xflat, dr["xw_in"], q="sp")
        if ("L1A" in phases) or ("L1B" in phases) or ("L1F" in phases):
            emit_mod(C, 1)
        V = l1_views(C)
        if ("L1A" in phases) or ("L1B" in phases):
            emit_l1_consts(C, V)
        if "L1A" in phases:
            P.memset(V.S, 0.0)
            P.memset(V.Lacc, 0.0)
            for v in range(NV):
                emit_l1_tile(C, V, v, False)
            if "L1B" not in phases:
                P.must_finish(P.dma(dout("Sloc", [128, 2048]), V.S, q="sp"))
                P.must_finish(P.dma(dout("Ltot", [128, 32]), V.Lacc, q="sp"))
        if "L1B" in phases:
            emit_sin(C, V)
            for v in range(NV):
                emit_l1_tile(C, V, v, True)
        if "L1F" in phases:
            for v in range(NV):
                if v == 0:
                    P.dma(C.xres[:, :, 0, 0:HA], dr["tail_in"].rearrange("p (k i) -> p k i", i=HA), q="sp")
                else:
                    P.copy(C.xres[:, :, v, 0:HA], C.xres[:, :, v - 1, NT:NT + HA])
            for v in range(NV):
                emit_ffn(C, 1, v)
            dr["yT_out"] = dout("yT_out", [128, 8 * TC])
            for v in range(NV):
                emit_final(C, v)
        else:
            P.must_finish(P.dma(dout("xw_out", [128, 8 * NV * WA]), xflat, q="sp"))
        P.emit()
    C.in_names = [k for k in dr.keys() if k in ishapes]
    nc._in_names = C.in_names
    return nc


def aview(C, off_bytes, shape, dt):
    es = 2 if dt == BF16 else 4
    n = 1
    for s in shape[1:]:
        n *= s
    nbytes = n * es
    assert off_bytes % 4 == 0 and off_bytes + nbytes <= 16896 * 4, (off_bytes, nbytes)
    base = C.arena[0:shape[0], off_bytes // 4: off_bytes // 4 + (nbytes + 3) // 4]
    if dt != F32:
        base = base.bitcast(dt)
        base = base[:, 0:n]
    if len(shape) == 2:
        return base
    if len(shape) == 3:
        return base.rearrange("p (a b) -> p a b", b=shape[2])
    if len(shape) == 4:
        return base.rearrange("p (a b c) -> p a b c", b=shape[2], c=shape[3])
    raise ValueError


def dbg(C, name, ap):
    if not getattr(C, "debug", False):
        return
    shp = [int(x) for x in ap.shape]
    o = C.nc.dram_tensor("dbg_" + name, shp, ap.dtype, kind="ExternalOutput").ap()
    C.P.must_finish(C.P.dma(o, ap, q="sp"))


def ps_f(C):
    t = C.psf[C.psfi % 4]
    C.psfi += 1
    return t


def ps_b(C):
    t = C.psb[C.psbi % len(C.psb)]
    C.psbi += 1
    return t


def tmp512(C):
    t = C.t512[C.ti % len(C.t512)]
    C.ti += 1
    return t


def smallcol(C, n=1):
    if C.smi + n > 64:
        C.smi = 0
    a = C.small[:, C.smi:C.smi + n]
    C.smi += n
    return a


def svcol(C, name, i=0, n=1, parts=128):
    a, _ = _SV[name]
    return C.svt[0:parts, a + i:a + i + n]


def rvrow(C, name, i=0, n=None):
    a, m = _RV[name]
    if n is None:
        n = m
    return C.rvt[:, a + i:a + i + n]


def load_w(C, dram_block, nelem, parts=128):
    buf = C.wring[C.wri % WRING]
    C.wri += 1
    C.P.dma(buf[0:parts, 0:nelem], dram_block[0:parts, 0:nelem], q="pool")
    return buf


def emit_setup(C):
    P, dr = C.P, C.dr
    P.dma(C.svt[:], dr["sv"], q="sp")
    P.dma(C.rvt[:], dr["rv"].partition_broadcast(128), q="sp")
    P.dma(C.cmf[:], dr["cm"][:, 0:512], q="sp")
    P.dma(C.cmb[:], dr["cm"], q="pool")
    P.memset(C.onesb[:], 1.0)
    P.memset(C.epsc[:, 0:1], 1e-6)
    P.memset(C.epsc[:, 1:2], 1e-5)


def emit_mod(C, l):
    P, dr = C.P, C.dr
    C.siluc = aview(C, 32768, [128, 1024], F32)
    C.junk = aview(C, 36864, [128, 1024], F32)
    P.dma(C.siluc, dr["crow"].partition_broadcast(128), q="sp")
    P.act(C.siluc, C.siluc, AF.Silu)
    wst = aview(C, 0, [128, 2, 4096], F32)
    for ld in range(12):
        buf = wst[:, ld % 2, :]
        P.dma(buf, dr["wmod%d" % l][ld], q="sp")
        for jj in range(4):
            j = ld * 4 + jj

            P.stt(C.junk, buf[:, jj * 1024:(jj + 1) * 1024], 1.0, C.siluc, ALU.mult, ALU.mult,
                  accum_out=C.modv[:, l, j:j + 1])
    P.tt(C.modv[:, l, :], C.modv[:, l, :], svcol(C, "bmod%d" % l, 0, 48), ALU.add)
    P.stt(C.modA[:, l, 0:8], C.modv[:, l, 8:16], 1.0, svcol(C, "nmix%d" % l, 0, 8), ALU.add, ALU.mult)
    P.stt(C.modA[:, l, 8:16], C.modv[:, l, 32:40], 1.0, svcol(C, "nffn%d" % l, 0, 8), ALU.add, ALU.mult)


def emit_norm(C, X, W, Acols, Bcols, out_bf):
    P = C.P
    for (s, n) in _tiles(W):
        ps = ps_f(C)
        for k in range(8):
            sq = C.b512[(k // 4) % 2]
            P.act(sq[:, k % 4, 0:n], X[:, k, s:s + n], AF.Square)
            P.matmul(ps[:, 0:n], C.onesb[:], sq[:, k % 4, 0:n], start=(k == 0), stop=(k == 7))
        rstd = C.statA
        P.act(rstd[:, 0:n], ps[:, 0:n], AF.Sqrt, bias=C.epsc[:, 0:1], scale=1.0 / D)
        P.recip(rstd[:, 0:n], rstd[:, 0:n])
        for k in range(8):
            t = tmp512(C)
            P.tt(t[:, 0:n], X[:, k, s:s + n], rstd[:, 0:n], ALU.mult)
            if Bcols is not None:
                P.act(out_bf[:, k, s:s + n], t[:, 0:n], AF.Identity, bias=Bcols[:, k:k + 1], scale=Acols[:, k:k + 1])
            else:
                P.act(out_bf[:, k, s:s + n], t[:, 0:n], AF.Identity, scale=Acols[:, k:k + 1])


def emit_ffn(C, l, v):
    P, dr = C.P, C.dr
    X = C.xres[:, :, v, :]
    hf = aview(C, 0, [128, 8, WA], BF16)
    U = aview(C, 16512, [128, 11, WA], BF16)
    Graw = [aview(C, 39216 + i * 4128, [128, WA], F32) for i in range(2)]
    cc = [aview(C, 47472 + i * 4128, [128, WA], F32) for i in range(2)]
    sgb = [aview(C, 55728 + i * 4128, [128, WA], F32) for i in range(2)]
    emit_norm(C, X, WA, C.modA[:, l, 8:16], C.modv[:, l, 24:32], hf)
    fa, _ = _SV["fcw%d" % l]
    flagc = svcol(C, "flag", v)
    gf = C.modv[:, l, 40:48]
    for half in range(2):
        P.memset(U[:, :, 0:2], 0.0)
        for jl in range(11):
            j = half * 11 + jl
            if j % 2 == 0 or jl == 0:
                wb = load_w(C, dr["wgv%d" % l][j // 2], 4096)
            wg = wb[:, (j % 2) * 2048: (j % 2) * 2048 + 1024].rearrange("p (k c) -> p k c", c=128)
            wv = wb[:, (j % 2) * 2048 + 1024: (j % 2) * 2048 + 2048].rearrange("p (k c) -> p k c", c=128)
            G = Graw[j % 2]
            for (s, n) in _tiles(WA):
                ps = ps_f(C)
                for k in range(8):
                    P.matmul(ps[:, 0:n], wg[:, k, :], hf[:, k, s:s + n], start=(k == 0), stop=(k == 7))
                P.copy(G[:, s:s + n], ps[:, 0:n], eng="act")
            P.ts(G[:, 0:HA], G[:, 0:HA], flagc, None, ALU.mult)
            c_ = cc[j % 2]
            L = WA - 2
            P.ts(c_[:, 2:WA], G[:, 2:WA], C.svt[:, fa + j * 3 + 2: fa + j * 3 + 3], svcol(C, "fcb%d" % l, j), ALU.mult, ALU.add)
            P.stt(c_[:, 2:WA], G[:, 1:1 + L], C.svt[:, fa + j * 3 + 1: fa + j * 3 + 2], c_[:, 2:WA], ALU.mult, ALU.add)
            P.stt(c_[:, 2:WA], G[:, 0:L], C.svt[:, fa + j * 3: fa + j * 3 + 1], c_[:, 2:WA], ALU.mult, ALU.add)
            sg = sgb[j % 2]
            P.act(sg[:, 2:WA], c_[:, 2:WA], AF.Silu)
            for (s, n) in _tiles(WA):
                ps = ps_f(C)
                for k in range(8):
                    P.matmul(ps[:, 0:n], wv[:, k, :], hf[:, k, s:s + n], start=(k == 0), stop=(k == 7))
                s2 = max(s, 2)
                n2 = s + n - s2
                P.tt(U[:, jl, s2:s2 + n2], ps[:, s2 - s:s2 - s + n2], sg[:, s2:s2 + n2], ALU.mult)
        for o in range(8):
            if o % 2 == 0:
                wb = load_w(C, dr["wdn%d" % l][half * 4 + o // 2], 2816)
            wd = wb[:, (o % 2) * 1408:(o % 2) * 1408 + 1408].rearrange("p (k c) -> p k c", c=128)
            for (s, n) in _tiles(WA):
                ps = ps_f(C)
                for jl in range(11):
                    P.matmul(ps[:, 0:n], wd[:, jl, :], U[:, jl, s:s + n], start=(jl == 0), stop=(jl == 10))
                P.stt(X[:, o, s:s + n], ps[:, 0:n], gf[:, o:o + 1], X[:, o, s:s + n], ALU.mult, ALU.add)


def emit_l0_tile(C, v):
    P, dr = C.P, C.dr
    X0 = aview(C, 0, [128, 8, WK], F32)
    h0 = aview(C, 40960, [128, 8, WK], BF16)
    QT = aview(C, 0, [64, 8, QW], BF16)
    glu = aview(C, 18432, [128, 4, GW], F32)
    KT = aview(C, 35456, [64, 2, WK], BF16)
    V = aview(C, 61440, [128, 10, 128], BF16)
    attnT = aview(C, 40960, [64, 8, WA], BF16)
    cv = aview(C, 0, [128, 4, WA], F32)
    confT = aview(C, 57472, [128, 4, WA], BF16)
    flagc = svcol(C, "flag", v)
    X = C.xres[:, :, v, :]
    c0 = v * NT
    xsrc = dr["xT"].rearrange("(k p) t -> p k t", p=128)
    P.dma(X0[:, 0:4, :], xsrc[:, 0:4, c0:c0 + WK], q="sp")
    P.dma(X0[:, 4:8, :], xsrc[:, 4:8, c0:c0 + WK], q="sp")
    P.dma(X[:, :, :], xsrc[:, :, c0 + HK - HA:c0 + WK], q="sp")
    posi = aview(C, 40960, [16, WK], F32).bitcast(I32)
    P.dma(posi, dr["pos"][:, c0:c0 + WK].partition_broadcast(16), q="sp")
    ang = C.rope[:, 0, :]
    P.copy(ang, posi)
    P.ts(ang, ang, svcol(C, "invf", 0, 1, 16), None, ALU.mult)
    sinT = C.rope[:, 1, :]
    kf = aview(C, 46080, [16, WK], F32)
    ki = posi

    def reduce_angle(dst, src, shift):
        C1 = 6.28125
        C2 = 2.0 * np.pi - 6.28125
        P.ts(dst, src, float(shift), None, ALU.add)
        P.ts(kf, dst, float(1.0 / (2.0 * np.pi)), None, ALU.mult)
        P.copy(ki, kf)
        P.copy(kf, ki)
        P.stt(dst, kf, -C1, dst, ALU.mult, ALU.add)
        P.stt(dst, kf, -C2, dst, ALU.mult, ALU.add)
        P.ts(kf, dst, float(np.pi), None, ALU.is_gt)
        P.stt(dst, kf, float(-2.0 * np.pi), dst, ALU.mult, ALU.add)
        P.ts(kf, dst, float(-np.pi), None, ALU.is_lt)
        P.stt(dst, kf, float(2.0 * np.pi), dst, ALU.mult, ALU.add)
        P.ts(dst, dst, 3.141592, -3.141592, ALU.min, ALU.max)

    reduce_angle(sinT, ang, 0.0)
    reduce_angle(ang, ang, float(0.5 * np.pi))
    P.act(sinT, sinT, AF.Sin)
    P.act(ang, ang, AF.Sin)
    P.ts(sinT, sinT, svcol(C, "sgn", 0, 1, 16), None, ALU.mult)
    cosT = ang
    emit_norm(C, X0, WK, C.modA[:, 0, 0:8], C.modv[:, 0, 0:8], h0)
    if v == 0:
        dbg(C, 'modv', C.modv[:])
        dbg(C, 'h0', h0[:, :, 0:WK])
        dbg(C, 'rope', C.rope)
    rperm = C.cmb[0:64, 512:528]
    for ld in range(2):
        wb = load_w(C, dr["wqk"][ld], 2560)
        for hh in range(5):
            h = ld * 5 + hh
            w = wb[:, hh * 512:(hh + 1) * 512].rearrange("p (k c) -> p k c", c=64)
            if h < 8:
                rng_ = [(128 + s, n) for (s, n) in _tiles(QW)]
            else:
                rng_ = _tiles(WK)
            for (s, n) in rng_:
                ps = ps_f(C)
                for k in range(8):
                    P.matmul(ps[0:64, 0:n], w[:, k, :], h0[:, k, s:s + n], start=(k == 0), stop=(k == 7))
                dest = QT[:, h, s - 128:s - 128 + n] if h < 8 else KT[:, h - 8, s:s + n]
                P.act(dest, ps[0:64, 0:n], AF.Identity, bias=svcol(C, "bqk", h, 1, 64))
                ps2 = ps_f(C)
                P.matmul(ps2[0:16, 0:n], rperm, dest)
                t1 = tmp512(C)
                t2 = tmp512(C)
                P.tt(t1[0:16, 0:n], ps2[0:16, 0:n], sinT[:, s:s + n], ALU.mult)
                P.tt(t2[0:16, 0:n], dest[0:16, :], cosT[:, s:s + n], ALU.mult)
                P.tt(dest[0:16, :], t1[0:16, 0:n], t2[0:16, 0:n], ALU.add)
    wb = load_w(C, dr["wv"][0], 1024)
    wvv = wb[:, 0:1024].rearrange("p (k c) -> p k c", c=128)
    for b in range(10):
        ps = ps_f(C)
        for k in range(8):
            P.matmul(ps[:, 0:128], h0[:, k, b * 128:(b + 1) * 128], wvv[:, k, :], start=(k == 0), stop=(k == 7))
        P.tt(V[:, b, :], ps[:, 0:128], rvrow(C, "bv"), ALU.add)
    for c in range(4):
        if c % 2 == 0:
            wb = load_w(C, dr["wcf"][c // 2], 4096)
        wa = wb[:, (c % 2) * 2048:(c % 2) * 2048 + 1024].rearrange("p (k c) -> p k c", c=128)
        wg = wb[:, (c % 2) * 2048 + 1024:(c % 2) * 2048 + 2048].rearrange("p (k c) -> p k c", c=128)
        for (s, n) in _tiles(GW):
            psa = ps_f(C)
            psg = ps_f(C)
            for k in range(8):
                P.matmul(psa[:, 0:n], wa[:, k, :], h0[:, k, G0 + s:G0 + s + n], start=(k == 0), stop=(k == 7))
            for k in range(8):
                P.matmul(psg[:, 0:n], wg[:, k, :], h0[:, k, G0 + s:G0 + s + n], start=(k == 0), stop=(k == 7))
            sgm = tmp512(C)
            P.act(sgm[:, 0:n], psg[:, 0:n], AF.Sigmoid, bias=svcol(C, "bcg", c))
            P.stt(glu[:, c, s:s + n], psa[:, 0:n], svcol(C, "bca", c), sgm[:, 0:n], ALU.add, ALU.mult)
        P.ts(glu[:, c, 0:40], glu[:, c, 0:40], flagc, None, ALU.mult)
    if v == 0:
        dbg(C, 'QT', QT)
        dbg(C, 'KT', KT)
        dbg(C, 'V', V)
        dbg(C, 'glu', glu)
    maskA = C.cmf[:, 0:256]
    maskH = C.cmf[0:8, 256:512]
    ident = C.cmb[:, 896:1024]
    negc = C.negc[:, v:v + 1]
    P.ts(negc, flagc, -1.0, -NEG, ALU.add, ALU.mult)
    for b in range(9):
        nq = 8 if b == 0 else 128
        qlo = b * 128 + (120 if b == 0 else 0)
        for h in range(8):
            kvh = h // 4
            ps = ps_f(C)
            P.matmul(ps[0:nq, 0:256], QT[:, h, qlo:qlo + nq], KT[:, kvh, b * 128:b * 128 + 256])
            Sm = tmp512(C)
            P.stt(Sm[0:nq, 0:256], ps[0:nq, 0:256], 0.125, maskH if b == 0 else maskA, ALU.mult, ALU.add)
            if b == 1:
                P.ts(Sm[:, 0:128], Sm[:, 0:128], negc, None, ALU.add)
            rmax = smallcol(C)
            P.reduce(rmax[0:nq], Sm[0:nq, 0:256], ALU.max)
            negm = smallcol(C)
            P.ts(negm[0:nq], rmax[0:nq], rvrow(C, "sinks", h, 1)[0:nq], -1.0, ALU.max, ALU.mult)
            Pm = C.b512[0][:, 0, 0:256] if (h % 2 == 0) else C.b512[1][:, 0, 0:256]
            rsum = smallcol(C)
            P.act(Pm[0:nq], Sm[0:nq, 0:256], AF.Exp, bias=negm[0:nq], accum_out=rsum[0:nq])
            stv = smallcol(C)
            P.act(stv[0:nq], negm[0:nq], AF.Exp, bias=rvrow(C, "sinks", h, 1)[0:nq])
            den = smallcol(C)
            P.tt(den[0:nq], rsum[0:nq], stv[0:nq], ALU.add)
            rinv = smallcol(C)
            P.recip(rinv[0:nq], den[0:nq])
            Pn = C.b512[0][:, 1, 0:256] if (h % 2 == 0) else C.b512[1][:, 1, 0:256]
            P.ts(Pn[0:nq], Pm[0:nq], rinv[0:nq], None, ALU.mult)
            pst = ps_b(C)
            P.transpose(pst[:, 0:nq], Pn[0:nq, 0:128], ident[0:nq, 0:nq])
            P.transpose(pst[:, 128:128 + nq], Pn[0:nq, 128:256], ident[0:nq, 0:nq])
            PT = C.b512[0][:, 2, 0:256] if (h % 2 == 0) else C.b512[1][:, 2, 0:256]
            P.copy(PT[:, 0:nq], pst[:, 0:nq], eng="act")
            P.copy(PT[:, 128:128 + nq], pst[:, 128:128 + nq], eng="act")
            pso = ps_f(C)
            P.matmul(pso[0:64, 0:nq], V[:, b, kvh * 64:(kvh + 1) * 64], PT[:, 0:nq], start=True, stop=False)
            P.matmul(pso[0:64, 0:nq], V[:, b + 1, kvh * 64:(kvh + 1) * 64], PT[:, 128:128 + nq], start=False, stop=True)
            i0 = 0 if b == 0 else HA + (b - 1) * 128
            P.copy(attnT[:, h, i0:i0 + nq], pso[0:64, 0:nq], eng="act")
    cwa, _ = _SV["ccw"]
    for c in range(4):
        acc = cv[:, c, :]
        P.ts(acc, glu[:, c, 2:2 + WA], C.svt[:, cwa + c * 31: cwa + c * 31 + 1], svcol(C, "ccb", c), ALU.mult, ALU.add)
        for k in range(1, 31):
            P.stt(acc, glu[:, c, 2 + k:2 + k + WA], C.svt[:, cwa + c * 31 + k: cwa + c * 31 + k + 1], acc, ALU.mult, ALU.add)
    for (s, n) in _tiles(WA):
        cvb = C.b512[0]
        sqb = C.b512[1]
        ps1 = ps_f(C)
        ps2 = ps_f(C)
        for c in range(4):
            P.act(cvb[:, c, 0:n], cv[:, c, s:s + n], AF.Identity)
            P.act(sqb[:, c, 0:n], cv[:, c, s:s + n], AF.Square)
        for c in range(4):
            P.matmul(ps1[:, 0:n], C.onesb[:], cvb[:, c, 0:n], start=(c == 0), stop=(c == 3))
        for c in range(4):
            P.matmul(ps2[:, 0:n], C.onesb[:], sqb[:, c, 0:n], start=(c == 0), stop=(c == 3))
        mu = C.statA
        var = C.statB
        P.ts(mu[:, 0:n], ps1[:, 0:n], 1.0 / 512, None, ALU.mult)
        P.tt(var[:, 0:n], mu[:, 0:n], mu[:, 0:n], ALU.mult)
        P.stt(var[:, 0:n], ps2[:, 0:n], 1.0 / 512, var[:, 0:n], ALU.mult, ALU.subtract)
        P.act(var[:, 0:n], var[:, 0:n], AF.Sqrt, bias=C.epsc[:, 1:2])
        P.recip(var[:, 0:n], var[:, 0:n])
        for c in range(4):
            t = tmp512(C)
            P.tt(t[:, 0:n], cv[:, c, s:s + n], mu[:, 0:n], ALU.subtract)
            P.tt(t[:, 0:n], t[:, 0:n], var[:, 0:n], ALU.mult)
            P.act(confT[:, c, s:s + n], t[:, 0:n], AF.Silu, bias=svcol(C, "lnb", c), scale=svcol(C, "lng", c))
    if v == 0:
        dbg(C, 'attnT', attnT)
        dbg(C, 'cv', cv)
        dbg(C, 'confT', confT)
    gm = C.modv[:, 0, 16:24]
    for o in range(8):
        if o % 2 == 0:
            wb = load_w(C, dr["wo"][o // 2], 3072)
        wo_a = wb[0:64, (o % 2) * 1536:(o % 2) * 1536 + 1024].rearrange("p (h c) -> p h c", c=128)
        wo_c = wb[:, (o % 2) * 1536 + 1024:(o % 2) * 1536 + 1536].rearrange("p (h c) -> p h c", c=128)
        for (s, n) in _tiles(WA):
            ps = ps_f(C)
            for h in range(8):
                P.matmul(ps[:, 0:n], wo_a[:, h, :], attnT[:, h, s:s + n], start=(h == 0), stop=False)
            for c in range(4):
                P.matmul(ps[:, 0:n], wo_c[:, c, :], confT[:, c, s:s + n], start=False, stop=(c == 3))
            P.stt(X[:, o, s:s + n], ps[:, 0:n], gm[:, o:o + 1], X[:, o, s:s + n], ALU.mult, ALU.add)
    if getattr(C, "stop_after_mixer", False):
        return
    emit_ffn(C, 0, v)


def l1_views(C):
    V = Ctx()
    V.h1sc = aview(C, 0, [128, 8, 264], BF16)
    V.S = aview(C, 4224, [128, 2048], F32)
    V.Sb = aview(C, 12416, [128, 2048], BF16)
    V.xtok = aview(C, 16512, [128, 2, 2048], BF16)
    V.Btok = aview(C, 24704, [128, 2, 512], BF16)
    V.BT = aview(C, 26752, [128, 4, 256], BF16)
    V.CT = aview(C, 28800, [128, 4, 256], BF16)
    V.gate2 = aview(C, 30848, [128, 2, 2048], BF16)
    V.y = aview(C, 39040, [128, 2048], F32)
    V.R1 = aview(C, 47232, [128, 2048], BF16)
    V.R2 = aview(C, 51328, [128, 2048], BF16)
    V.yT = aview(C, 55424, [128, 16, 256], BF16)
    ax = C.aux
    V.CBm = ax[:, 0:512].rearrange("p (g l) -> p g l", l=128)
    V.Lm = [ax[:, 512 + 128 * i: 640 + 128 * i] for i in range(3)]
    V.Mb = [ax[:, 896 + 64 * i: 960 + 64 * i].bitcast(BF16) for i in range(3)]
    V.sil = [ax[:, 1088 + 128 * i: 1216 + 128 * i].bitcast(BF16) for i in range(2)]
    V.ssm = [ax[:, 1344 + 32 * i: 1376 + 32 * i] for i in range(16)]
    V.sbf = [ax[:, 1856 + 16 * i: 1872 + 16 * i].bitcast(BF16) for i in range(4)]
    V.Ab = ax[:, 1920:1952]
    V.Lacc = ax[:, 1984:2016]
    V.Lall = ax[:, 2048:2304].rearrange("p (j h) -> p j h", h=32)
    V.ssq = ax[:, 2304:2308]
    V.rs = ax[:, 2308:2312]
    V.rawtail = ax[:, 2320:2392].rearrange("p (c t) -> p c t", t=3)
    return V


def bc3(ap2, n):
    return ap2.unsqueeze(2).broadcast_to([ap2.shape[0], ap2.shape[1], n])


def emit_l1_consts(C, V):
    P = C.P
    P.act(V.Ab, rvrow(C, "alog"), AF.Exp)
    P.ts(V.Ab, V.Ab, -1.0, None, ALU.mult)


def emit_l1_tile(C, V, v, full):
    P, dr = C.P, C.dr
    X = C.xres[:, :, v, :]
    flagc = svcol(C, "flag", v)
    ident = C.cmb[:, 896:1024]
    Ubf = C.cmb[:, 640:768]
    NEGMb = C.cmb[:, 768:896]
    scw_a, _ = _SV["scw"]
    gm1 = C.modv[:, 1, 16:24]
    NW = 259
    (dtr, dtt, aa, acs, nacs, tot, wv, eacs, dcb, ee, dd) = V.ssm[0:11]
    ahi, alo, acshi, acslo = V.sbf
    for sc in range(4):
        m0 = HA + sc * 256
        w0 = m0 - 3
        emit_norm(C, X[:, :, w0:w0 + NW], NW, C.modA[:, 1, 0:8], C.modv[:, 1, 0:8], V.h1sc)
        for cc in range(24):
            if cc % 4 == 0:
                wb = load_w(C, dr["wxbc"][cc // 4], 4096)
            if (not full) and cc >= 20:
                continue
            w = wb[:, (cc % 4) * 1024:(cc % 4 + 1) * 1024].rearrange("p (k c) -> p k c", c=128)
            ps = ps_f(C)
            for k in range(8):
                P.matmul(ps[:, 0:NW], w[:, k, :], V.h1sc[:, k, 0:NW], start=(k == 0), stop=(k == 7))
            raw = tmp512(C)
            P.copy(raw[:, 0:NW], ps[:, 0:NW], eng="act")
            if sc == 0:
                P.ts(raw[:, 0:3], raw[:, 0:3], flagc, None, ALU.mult)
            elif full:
                P.copy(raw[:, 0:3], V.rawtail[:, cc, :])
            if full:
                P.copy(V.rawtail[:, cc, :], raw[:, 256:259])
            acc = tmp512(C)
            P.ts(acc[:, 0:256], raw[:, 3:259], C.svt[:, scw_a + cc * 4 + 3:scw_a + cc * 4 + 4], svcol(C, "scb", cc), ALU.mult, ALU.add)
            for k in (2, 1, 0):
                P.stt(acc[:, 0:256], raw[:, k:k + 256], C.svt[:, scw_a + cc * 4 + k:scw_a + cc * 4 + k + 1], acc[:, 0:256], ALU.mult, ALU.add)
            if cc < 16:
                dst = V.sil[cc % 2]
            elif cc < 20:
                dst = V.BT[:, cc - 16, :]
            else:
                dst = V.CT[:, cc - 20, :]
            P.act(dst, acc[:, 0:256], AF.Silu)
            if cc < 20:
                pst = ps_b(C)
                for ch in range(2):
                    P.transpose(pst[:, ch * 128:(ch + 1) * 128], dst[:, ch * 128:(ch + 1) * 128], ident)
                src = pst[:, 0:256].rearrange("p (c t) -> p c t", t=128)
                if cc < 16:
                    P.copy(V.xtok[:, :, cc * 128:(cc + 1) * 128], src, eng=("act" if cc % 2 else "dve"))
                else:
                    P.copy(V.Btok[:, :, (cc - 16) * 128:(cc - 15) * 128], src, eng="dve")
        if full:
            for zt in range(4):
                wb = load_w(C, dr["wz"][zt], 4096)
                wz = wb[:, 0:4096].rearrange("p (k c) -> p k c", c=512)
                for ch in range(2):
                    ps = ps_f(C)
                    for k in range(8):
                        P.matmul(ps[:, 0:512], V.h1sc[:, k, 3 + ch * 128:3 + (ch + 1) * 128], wz[:, k, :], start=(k == 0), stop=(k == 7))
                    P.act(V.gate2[:, ch, zt * 512:(zt + 1) * 512], ps[:, 0:512], AF.Silu)
        wb = load_w(C, dr["wdt"][0], 256)
        wdt = wb[:, 0:256].rearrange("p (k c) -> p k c", c=32)
        for ch in range(2):
            ps = ps_f(C)
            for k in range(8):
                P.matmul(ps[:, 0:32], V.h1sc[:, k, 3 + ch * 128:3 + (ch + 1) * 128], wdt[:, k, :], start=(k == 0), stop=(k == 7))
            P.tt(dtr, ps[:, 0:32], rvrow(C, "dtb"), ALU.add)
            P.act(ee, dtr, AF.Exp)
            P.act(dtt, ee, AF.Ln, bias=svcol(C, "one"))
            P.tt(aa, dtt, V.Ab, ALU.mult)
            P.copy(ahi, aa, eng="act")
            P.tt(alo, aa, ahi, ALU.subtract)
            ps2 = ps_f(C)
            P.matmul(ps2[:, 0:32], Ubf, ahi, start=True, stop=False)
            P.matmul(ps2[:, 0:32], Ubf, alo, start=False, stop=True)
            P.matmul(ps2[:, 32:64], C.onesb[:], ahi, start=True, stop=False)
            P.matmul(ps2[:, 32:64], C.onesb[:], alo, start=False, stop=True)
            P.copy(acs, ps2[:, 0:32])
            P.ts(nacs, ps2[:, 0:32], -1.0, None, ALU.mult)
            P.copy(tot, ps2[:, 32:64])
            P.tt(dd, tot, acs, ALU.subtract)
            P.act(dd, dd, AF.Exp)
            P.tt(wv, dd, dtt, ALU.mult)
            P.act(dcb, tot, AF.Exp)
            if not full:
                P.tt(V.Lacc, V.Lacc, tot, ALU.add)
            xt3 = V.xtok[:, ch, :].rearrange("p (h d) -> p h d", d=64)
            if full:
                P.act(eacs, acs, AF.Exp)
                P.copy(acshi, acs, eng="act")
                P.tt(acslo, acs, acshi, ALU.subtract)
                for g in range(4):
                    ps = ps_f(C)
                    P.matmul(ps[:, 0:128], V.BT[:, g, ch * 128:(ch + 1) * 128], V.CT[:, g, ch * 128:(ch + 1) * 128])
                    P.tt(V.CBm[:, g, :], ps[:, 0:128], Ubf, ALU.mult)
                xD = V.R1
                P.tt(xD.rearrange("p (h d) -> p h d", d=64), xt3, bc3(rvrow(C, "dsk"), 64), ALU.mult)
                for g in range(4):
                    psy = C.psf[4]
                    pso = C.psf[5]
                    P.matmul(pso[:, 0:512], V.CT[:, g, ch * 128:(ch + 1) * 128], V.Sb[:, g * 512:(g + 1) * 512])
                    P.matmul(psy[:, 0:512], ident, xD[:, g * 512:(g + 1) * 512], start=True, stop=False, skip_group_check=True)
                    for j in range(8):
                        h = g * 8 + j
                        psL = ps_f(C)
                        P.matmul(psL[:, 0:128], acshi[:, h:h + 1].broadcast_to([128, 128]), ident, start=True, stop=False)
                        P.matmul(psL[:, 0:128], acslo[:, h:h + 1].broadcast_to([128, 128]), ident, start=False, stop=False)
                        P.matmul(psL[:, 0:128], ident, NEGMb, start=False, stop=True)
                        Lm = V.Lm[h % 3]
                        P.act(Lm, psL[:, 0:128], AF.Exp, bias=nacs[:, h:h + 1])
                        Mb = V.Mb[h % 3]
                        P.stt(Mb, Lm, dtt[:, h:h + 1], V.CBm[:, g, :], ALU.mult, ALU.mult)
                        P.matmul(psy[:, j * 64:(j + 1) * 64], Mb, V.xtok[:, ch, h * 64:(h + 1) * 64], start=False, stop=True, skip_group_check=True)
                    tmp = tmp512(C)
                    P.tt(tmp[:, 0:512].rearrange("p (h d) -> p h d", d=64), pso[:, 0:512].rearrange("p (h d) -> p h d", d=64),
                         bc3(eacs[:, g * 8:(g + 1) * 8], 64), ALU.mult)
                    P.tt(V.y[:, g * 512:(g + 1) * 512], tmp[:, 0:512], psy[:, 0:512], ALU.add)
            xw = V.R2
            P.tt(xw.rearrange("p (h d) -> p h d", d=64), xt3, bc3(wv, 64), ALU.mult)
            for g in range(4):
                ps = ps_f(C)
                P.matmul(ps[:, 0:512], V.Btok[:, ch, g * 128:(g + 1) * 128], xw[:, g * 512:(g + 1) * 512])
                Sg = V.S[:, g * 512:(g + 1) * 512]
                Sg3 = Sg.rearrange("p (h d) -> p h d", d=64)
                P.tt(Sg3, Sg3, bc3(dcb[:, g * 8:(g + 1) * 8], 64), ALU.mult)
                P.tt(Sg, Sg, ps[:, 0:512], ALU.add)
            if full:
                P.copy(V.Sb, V.S, eng="act")
                for g in range(4):
                    yg = V.y[:, g * 512:(g + 1) * 512]
                    P.tt(yg, yg, V.gate2[:, ch, g * 512:(g + 1) * 512], ALU.mult)
                    junk = tmp512(C)
                    P.act(junk[:, 0:512], yg, AF.Square, accum_out=V.ssq[:, g:g + 1])
                P.act(V.rs, V.ssq, AF.Sqrt, bias=C.epsc[:, 0:1], scale=1.0 / 512)
                P.recip(V.rs, V.rs)
                yn = V.R1
                for g in range(4):
                    P.ts(yn[:, g * 512:(g + 1) * 512], V.y[:, g * 512:(g + 1) * 512], V.rs[:, g:g + 1], None, ALU.mult)
                for cc in range(16):
                    pst = ps_b(C)
                    P.transpose(pst[:, 0:128], yn[:, cc * 128:(cc + 1) * 128], ident)
                    if cc % 2:
                        P.act(V.yT[:, cc, ch * 128:(ch + 1) * 128], pst[:, 0:128], AF.Identity, scale=svcol(C, "ngc", cc))
                    else:
                        P.ts(V.yT[:, cc, ch * 128:(ch + 1) * 128], pst[:, 0:128], svcol(C, "ngc", cc), None, ALU.mult)
        if full:
            for o in range(8):
                if o % 2 == 0:
                    wb = load_w(C, dr["woo"][o // 2], 4096)
                wo = wb[:, (o % 2) * 2048:(o % 2 + 1) * 2048].rearrange("p (k c) -> p k c", c=128)
                ps = ps_f(C)
                for cc in range(16):
                    P.matmul(ps[:, 0:256], wo[:, cc, :], V.yT[:, cc, :], start=(cc == 0), stop=(cc == 15))
                P.stt(X[:, o, m0:m0 + 256], ps[:, 0:256], gm1[:, o:o + 1], X[:, o, m0:m0 + 256], ALU.mult, ALU.add)


def emit_sin(C, V):
    P, dr = C.P, C.dr
    ma, _ = _SV["msk"]
    P.dma(V.Lall, dr["Lall"].rearrange("j p h -> p j h"), q="sp")
    P.memset(V.S, 0.0)
    Dj = V.ssm[11]
    ej = V.ssm[12]
    tmpS = V.y
    for j in range(8):
        P.memset(Dj, 0.0)
        for i in range(8):
            P.stt(Dj, V.Lall[:, i, :], C.svt[:, ma + j * 8 + i:ma + j * 8 + i + 1], Dj, ALU.mult, ALU.add)
        P.act(ej, Dj, AF.Exp)
        P.ts(ej, ej, C.svt[:, ma + 64 + j:ma + 65 + j], None, ALU.mult)
        P.dma(tmpS, dr["Sall"][j], q="sp")
        t3 = tmpS.rearrange("p (h d) -> p h d", d=64)
        P.tt(t3, t3, bc3(ej, 64), ALU.mult)
        P.tt(V.S, V.S, tmpS, ALU.add)
    P.copy(V.Sb, V.S, eng="act")


def emit_final(C, v):
    P, dr = C.P, C.dr
    X = C.xres[:, :, v, :]
    outv = dr["yT_out"].rearrange("p (k t) -> p k t", t=TC)
    fa, _ = _SV["fnorm"]
    ob = aview(C, 0, [128, 2, 8, 512], F32)
    i = 0
    for (s, n) in _tiles(NT):
        ps = ps_f(C)
        for k in range(8):
            sq = C.b512[(k // 4) % 2]
            P.act(sq[:, k % 4, 0:n], X[:, k, HA + s:HA + s + n], AF.Square)
            P.matmul(ps[:, 0:n], C.onesb[:], sq[:, k % 4, 0:n], start=(k == 0), stop=(k == 7))
        rstd = C.statA
        P.act(rstd[:, 0:n], ps[:, 0:n], AF.Sqrt, bias=C.epsc[:, 0:1], scale=1.0 / D)
        P.recip(rstd[:, 0:n], rstd[:, 0:n])
        o = ob[:, i % 2]
        i += 1
        for k in range(8):
            P.stt(o[:, k, 0:n], X[:, k, HA + s:HA + s + n], C.svt[:, fa + k:fa + k + 1], rstd[:, 0:n], ALU.mult, ALU.mult)
        ev = P.dma(outv[:, :, v * NT + s:v * NT + s + n], o[:, :, 0:n], q="sp")
        P.must_finish(ev)


def make_inmaps(inp, W=None):
    if W is None:
        W = pack_weights(inp)
    x = np.asarray(inp["x"], np.float32)[0]
    pos = np.asarray(inp["positions"], np.int32)[0]
    maps = []
    for c in range(NCORES):
        t0 = c * TC
        xw = np.zeros((HK + TC, D), np.float32)
        pw = np.zeros((1, HK + TC), np.int32)
        lo = max(0, t0 - HK)
        xw[HK - (t0 - lo):] = x[lo:t0 + TC]
        pw[0, HK - (t0 - lo):] = pos[lo:t0 + TC]
        m = dict(W)
        sv = W["sv"].copy()
        a, n = _SV["flag"]
        sv[:, a:a + n] = 1.0
        if c == 0:
            sv[:, a] = 0.0
        m["sv"] = sv
        m["xT"] = np.ascontiguousarray(xw.T)
        m["pos"] = pw
        maps.append(m)
    return maps


_PROG_CACHE = {}


def _prog(key):
    if key not in _PROG_CACHE:
        _PROG_CACHE[key] = build_program(list(key))
    return _PROG_CACHE[key]


def _run(nc, maps):
    names = nc._in_names
    in_maps = [{k: m[k] for k in names} for m in maps]
    res = run_bass_kernel_spmd(nc, in_maps, core_ids=list(range(NCORES)))
    return res.results


def kernel(**inp):
    maps = make_inmaps(inp)
    ma, _ = _SV["msk"]
    for c in range(NCORES):
        sv = maps[c]["sv"]
        for j in range(8):
            for i in range(8):
                sv[:, ma + j * 8 + i] = 1.0 if (j < i < c) else 0.0
            sv[:, ma + 64 + j] = 1.0 if j < c else 0.0
    r1 = _run(_prog(("L0", "L1A")), maps)
    Sall = np.stack([r["Sloc"] for r in r1])
    Lall = np.stack([r["Ltot"] for r in r1])
    for c in range(NCORES):
        maps[c]["xw_in"] = r1[c]["xw_out"]
        maps[c]["Sall"] = Sall
        maps[c]["Lall"] = Lall
    r2 = _run(_prog(("L1B",)), maps)
    for c in range(NCORES):
        maps[c]["xw_in"] = r2[c]["xw_out"]
        if c == 0:
            maps[c]["tail_in"] = np.zeros((128, 64), np.float32)
        else:
            xw = r2[c - 1]["xw_out"].reshape(128, 8, NV, WA)
            maps[c]["tail_in"] = np.ascontiguousarray(xw[:, :, NV - 1, NT:NT + HA].reshape(128, 64))
    r3 = _run(_prog(("L1F",)), maps)
    out = np.zeros((1, SEQ, D), np.float32)
    for c in range(NCORES):
        yT = r3[c]["yT_out"].reshape(128, 8, TC)
        out[0, c * TC:(c + 1) * TC, :] = yT.transpose(2, 1, 0).reshape(TC, D)
    return out
```
